# Optimizing a Trainium2 kernel written in Bass

```python
import jax, jax.numpy as jnp
from jax import lax
import numpy as np

D_MODEL = 2048
BATCH = 4
SEQ = 4096
DEPTH = 2

HEAD_DIM = 128
EPS = 1e-6
ROPE_THETA = 10000.0
N_EVEN = (DEPTH + 1) // 2
N_ODD = DEPTH // 2

POOL_WINDOWS = (2, 4, 8, 16)
POOL_WIDTH = D_MODEL // 2
POOL_GROUP = POOL_WIDTH // len(POOL_WINDOWS)

DIL_PATTERNS = ((128, 1), (512, 4), (2048, 16))
N_DIL_GROUPS = len(DIL_PATTERNS)
DIL_HEADS = (D_MODEL - POOL_WIDTH) // HEAD_DIM
DIL_QBLOCK = 128
AB_IN = POOL_WIDTH + 3 * N_DIL_GROUPS * DIL_HEADS * HEAD_DIM

SB_HEADS = D_MODEL // HEAD_DIM // 2
SB_QBLOCK = 128
MOBA_HEADS = D_MODEL // HEAD_DIM - SB_HEADS
MOBA_BLOCK = 256
MOBA_TOPK = 3
MOBA_QCHUNK = 16
CD_IN = 3 * (SB_HEADS + MOBA_HEADS) * HEAD_DIM

FFN_HIDDEN = -(-8 * D_MODEL // (3 * 256)) * 256

kernel_name = "hybrid_pool_dilated_stickbreak_moba_block"


def rms_norm(x, g):
    xf = x.astype(jnp.float32)
    y = xf * lax.rsqrt(jnp.mean(xf * xf, axis=-1, keepdims=True) + EPS)
    return (y * g.astype(jnp.float32)).astype(x.dtype)


def rope_tables(seq):
    inv = 1.0 / (ROPE_THETA ** (jnp.arange(0, HEAD_DIM, 2, dtype=jnp.float32) / HEAD_DIM))
    ang = jnp.arange(seq, dtype=jnp.float32)[:, None] * inv[None, :]
    return jnp.cos(ang), jnp.sin(ang)


def apply_rope(x, cos, sin):
    x1, x2 = jnp.split(x.astype(jnp.float32), 2, axis=-1)
    return jnp.concatenate([x1 * cos - x2 * sin, x2 * cos + x1 * sin], axis=-1).astype(x.dtype)


def pool_mixer(u, pool_w, pool_scale):
    Bn, S, _ = u.shape
    uf = u.astype(jnp.float32).reshape(Bn, S, len(POOL_WINDOWS), POOL_GROUP)
    c = jnp.pad(jnp.cumsum(uf, axis=1), ((0, 0), (1, 0), (0, 0), (0, 0)))
    t = jnp.arange(S)
    pooled = []
    for g, w in enumerate(POOL_WINDOWS):
        lo = jnp.maximum(t + 1 - w, 0)
        cg = c[:, :, g]
        cnt = (t + 1 - lo).astype(jnp.float32)[None, :, None]
        pooled.append((cg[:, 1:] - cg[:, lo]) / cnt - uf[:, :, g])
    pooled = jnp.stack(pooled, axis=2)
    y = jnp.einsum('bsgc,gcd->bsgd', pooled.astype(u.dtype), pool_w)
    return y.reshape(Bn, S, POOL_WIDTH) * pool_scale


def dilated_branch(q, k, v, dil, band):
    Bn, H, S, dh = q.shape
    L = S // dil
    nblk = -(-L // DIL_QBLOCK)
    Lp = nblk * DIL_QBLOCK
    QB = DIL_QBLOCK

    def to_blocks(x):
        xs = x.reshape(Bn, H, L, dil, dh).transpose(0, 1, 3, 2, 4)
        xs = jnp.pad(xs, ((0, 0), (0, 0), (0, 0), (0, Lp - L), (0, 0)))
        return xs.reshape(Bn, H, dil, nblk, QB, dh)

    qb, kb, vb = to_blocks(q), to_blocks(k), to_blocks(v)
    prev = lambda x: jnp.pad(x, ((0, 0),) * 3 + ((1, 0), (0, 0), (0, 0)))[:, :, :, :-1]
    kk = jnp.concatenate([prev(kb), kb], axis=4)
    vv = jnp.concatenate([prev(vb), vb], axis=4)
    s = jnp.einsum('bhrnqe,bhrnke->bhrnqk', qb, kk).astype(jnp.float32) * (HEAD_DIM ** -0.5)
    n = jnp.arange(nblk)[:, None, None]
    qg = n * QB + jnp.arange(QB)[None, :, None]
    kg = n * QB + jnp.arange(2 * QB)[None, None, :] - QB
    dist = qg - kg
    mask = (dist >= 0) & (dist <= band) & (kg >= 0)
    s = jnp.where(mask, s, -jnp.inf)
    m = jnp.max(s, axis=-1, keepdims=True)
    p = jnp.exp(s - m)
    den = jnp.sum(p, axis=-1, keepdims=True)
    o = jnp.einsum('bhrnqk,bhrnke->bhrnqe', (p / den).astype(v.dtype), vv)
    lse = (m + jnp.log(den))[..., 0]
    o = o.reshape(Bn, H, dil, Lp, dh)[:, :, :, :L].transpose(0, 1, 3, 2, 4).reshape(Bn, H, S, dh)
    lse = lse.reshape(Bn, H, dil, Lp)[:, :, :, :L].transpose(0, 1, 3, 2).reshape(Bn, H, S)
    return o, lse


def stick_breaking(q, k, v):
    Bn, H, S, dh = q.shape
    nblk = S // SB_QBLOCK
    qb = q.reshape(Bn, H, nblk, SB_QBLOCK, dh).transpose(2, 0, 1, 3, 4)
    kpos = jnp.arange(S)

    def block(args):
        qi, i = args
        z = jnp.einsum('bhqe,bhke->bhqk', qi, k).astype(jnp.float32) * (HEAD_DIM ** -0.5)
        qpos = i * SB_QBLOCK + jnp.arange(SB_QBLOCK)
        mask = kpos[None, :] < qpos[:, None]
        log1m = jnp.where(mask, -jax.nn.softplus(z), 0.0)
        suffix = lax.cumsum(log1m, axis=3, reverse=True) - log1m
        a = jnp.where(mask, jnp.exp(jax.nn.log_sigmoid(z) + suffix), 0.0)
        return jnp.einsum('bhqk,bhke->bhqe', a.astype(v.dtype), v)

    out = lax.map(block, (qb, jnp.arange(nblk)))
    return out.transpose(1, 2, 0, 3, 4).reshape(Bn, H, S, dh)


def moba_attention(q, k, v):
    Bn, H, S, dh = q.shape
    nb = -(-S // MOBA_BLOCK)
    Sp = nb * MOBA_BLOCK
    padk = ((0, 0), (0, 0), (0, Sp - S), (0, 0))
    kblk = jnp.pad(k, padk).reshape(Bn, H, nb, MOBA_BLOCK, dh)
    vblk = jnp.pad(v, padk).reshape(Bn, H, nb, MOBA_BLOCK, dh)
    kmean = jnp.mean(kblk.astype(jnp.float32), axis=3)
    topk = min(MOBA_TOPK, nb)
    nq = S // MOBA_QCHUNK
    qc = q.reshape(Bn, H, nq, MOBA_QCHUNK, dh).transpose(2, 0, 1, 3, 4)
    bi = jnp.arange(Bn)[:, None, None, None]
    hi = jnp.arange(H)[None, :, None, None]
    scale = HEAD_DIM ** -0.5

    def chunk(args):
        qi, c = args
        qpos = c * MOBA_QCHUNK + jnp.arange(MOBA_QCHUNK)
        ob = (c * MOBA_QCHUNK) // MOBA_BLOCK
        gate = jnp.einsum('bhqe,bhne->bhqn', qi.astype(jnp.float32), kmean)
        past = jnp.arange(nb)[None, :] < ob
        gate = jnp.where(past, gate, -jnp.inf)
        gval, gidx = lax.top_k(gate, topk)
        valid = jnp.isfinite(gval)
        ksel = kblk[bi, hi, gidx]
        vsel = vblk[bi, hi, gidx]
        s_sel = jnp.einsum('bhqe,bhqnke->bhqnk', qi, ksel).astype(jnp.float32) * scale
        s_sel = jnp.where(valid[..., None], s_sel, -jnp.inf).reshape(Bn, H, MOBA_QCHUNK, topk * MOBA_BLOCK)
        kown = lax.dynamic_index_in_dim(kblk, ob, axis=2, keepdims=False)
        vown = lax.dynamic_index_in_dim(vblk, ob, axis=2, keepdims=False)
        s_own = jnp.einsum('bhqe,bhke->bhqk', qi, kown).astype(jnp.float32) * scale
        kpos = ob * MOBA_BLOCK + jnp.arange(MOBA_BLOCK)
        s_own = jnp.where(kpos[None, :] <= qpos[:, None], s_own, -jnp.inf)
        p = jax.nn.softmax(jnp.concatenate([s_sel, s_own], axis=-1), axis=-1).astype(v.dtype)
        p_sel = p[..., :topk * MOBA_BLOCK].reshape(Bn, H, MOBA_QCHUNK, topk, MOBA_BLOCK)
        p_own = p[..., topk * MOBA_BLOCK:]
        return (jnp.einsum('bhqnk,bhqnke->bhqe', p_sel, vsel)
                + jnp.einsum('bhqk,bhke->bhqe', p_own, vown))

    out = lax.map(chunk, (qc, jnp.arange(nq)))
    return out.transpose(1, 2, 0, 3, 4).reshape(Bn, H, S, dh)


def mix_ab(h, w_in, pool_w, pool_scale, w_out, cos, sin):
    Bn, S, _ = h.shape
    proj = h @ w_in
    a_out = pool_mixer(proj[..., :POOL_WIDTH], pool_w, pool_scale)
    qkv = proj[..., POOL_WIDTH:].reshape(Bn, S, 3, N_DIL_GROUPS, DIL_HEADS, HEAD_DIM)
    qkv = qkv.transpose(2, 3, 0, 4, 1, 5)
    outs, lses = [], []
    for g, (window, dil) in enumerate(DIL_PATTERNS):
        q = apply_rope(qkv[0, g], cos, sin)
        k = apply_rope(qkv[1, g], cos, sin)
        o, lse = dilated_branch(q, k, qkv[2, g], dil, window // dil)
        outs.append(o)
        lses.append(lse)
    wts = jax.nn.softmax(jnp.stack(lses, axis=0), axis=0)
    o = jnp.sum(wts[..., None] * jnp.stack(outs, axis=0).astype(jnp.float32), axis=0)
    b_out = o.transpose(0, 2, 1, 3).reshape(Bn, S, DIL_HEADS * HEAD_DIM).astype(h.dtype)
    return jnp.concatenate([a_out, b_out], axis=-1) @ w_out


def mix_cd(h, w_in, w_out, cos, sin):
    Bn, S, _ = h.shape
    proj = h @ w_in
    c_w = 3 * SB_HEADS * HEAD_DIM
    sb = proj[..., :c_w].reshape(Bn, S, 3, SB_HEADS, HEAD_DIM).transpose(2, 0, 3, 1, 4)
    mb = proj[..., c_w:].reshape(Bn, S, 3, MOBA_HEADS, HEAD_DIM).transpose(2, 0, 3, 1, 4)
    c_out = stick_breaking(sb[0], sb[1], sb[2])
    d_out = moba_attention(apply_rope(mb[0], cos, sin), apply_rope(mb[1], cos, sin), mb[2])
    cat = jnp.concatenate([c_out, d_out], axis=1)
    return cat.transpose(0, 2, 1, 3).reshape(Bn, S, D_MODEL) @ w_out


def swiglu(h, w_gate, w_up, w_down):
    return (jax.nn.silu(h @ w_gate) * (h @ w_up)) @ w_down


def setup_inputs(seed: int = 0) -> dict:
    key = jax.random.key(seed)
    ks = jax.random.split(key, 12)
    nrm = lambda k, shape, fan: jax.random.normal(k, shape, jnp.float32) * (fan ** -0.5)
    return {
        "x": jax.random.normal(ks[0], (BATCH, SEQ, D_MODEL), jnp.float32),
        "norm_gains": 1.0 + 0.05 * jax.random.normal(ks[1], (DEPTH, 4, D_MODEL), jnp.float32),
        "w_in_ab": nrm(ks[2], (N_EVEN, D_MODEL, AB_IN), D_MODEL),
        "pool_w": nrm(ks[3], (N_EVEN, len(POOL_WINDOWS), POOL_GROUP, POOL_GROUP), POOL_GROUP),
        "pool_scale": 1.0 + 0.1 * jax.random.normal(ks[4], (N_EVEN, POOL_WIDTH), jnp.float32),
        "w_out_ab": nrm(ks[5], (N_EVEN, D_MODEL, D_MODEL), D_MODEL),
        "w_in_cd": nrm(ks[6], (N_ODD, D_MODEL, CD_IN), D_MODEL),
        "w_out_cd": nrm(ks[7], (N_ODD, D_MODEL, D_MODEL), D_MODEL),
        "ffn_gate": nrm(ks[8], (DEPTH, D_MODEL, FFN_HIDDEN), D_MODEL),
        "ffn_up": nrm(ks[9], (DEPTH, D_MODEL, FFN_HIDDEN), D_MODEL),
        "ffn_down": nrm(ks[10], (DEPTH, FFN_HIDDEN, D_MODEL), FFN_HIDDEN),
    }


def reference(x, norm_gains, w_in_ab, pool_w, pool_scale, w_out_ab, w_in_cd, w_out_cd,
              ffn_gate, ffn_up, ffn_down):
    cos, sin = rope_tables(x.shape[1])
    for layer in range(DEPTH):
        g = norm_gains[layer]
        hn = rms_norm(x, g[0])
        i = layer // 2
        if layer % 2 == 0:
            y = mix_ab(hn, w_in_ab[i], pool_w[i], pool_scale[i], w_out_ab[i], cos, sin)
        else:
            y = mix_cd(hn, w_in_cd[i], w_out_cd[i], cos, sin)
        x = x + rms_norm(y, g[1])
        f = swiglu(rms_norm(x, g[2]), ffn_gate[layer], ffn_up[layer], ffn_down[layer])
        x = x + rms_norm(f, g[3])
    return x
```

```python
import numpy as np
import ml_dtypes
import concourse.bass as bass
import concourse.mybir as mybir
from concourse.bass_utils import run_bass_kernel_spmd

F32 = mybir.dt.float32
BF16 = mybir.dt.bfloat16
AF = mybir.ActivationFunctionType
ALU = mybir.AluOpType
AX = mybir.AxisListType

D = 2048
T = 4096
SEQ = 4096
NKC = 16
FF = 5632
NFC = 44
EPS = 1e-6
SCALE = 128 ** -0.5
NEG = -30000.0
DILS = (1, 4, 16)
NDMA_SEM = 8
STRICT_SAME_ENGINE = True
SB_BASE = 16512
SB_LIMIT = 229000


class Op:
    __slots__ = ("eng", "fn", "reads", "writes", "dma", "deps", "pos", "signal",
                 "semval", "waits", "dma_slot", "dma_val", "idx")


class Sched:
    ENGS = ("pe", "act", "dve", "pool", "sp")

    def __init__(self, nc):
        self.nc = nc
        self.ops = []
        self.last_writer = {}
        self.readers = {}
        self.phase_key = None

    def op(self, eng, fn, reads=(), writes=(), dma=False):
        o = Op()
        o.eng, o.fn, o.dma = eng, fn, dma
        o.reads = tuple(reads) + (("__phase__",) if self.phase_key else ())
        o.writes = tuple(writes)
        o.idx = len(self.ops)
        deps = set()
        for k in o.reads:
            w = self.last_writer.get(k)
            if w is not None:
                deps.add(w)
        for k in o.writes:
            w = self.last_writer.get(k)
            if w is not None:
                deps.add(w)
            deps.update(self.readers.get(k, ()))
        deps.discard(o.idx)
        o.deps = deps
        for k in o.reads:
            self.readers.setdefault(k, []).append(o.idx)
        for k in o.writes:
            self.last_writer[k] = o.idx
            self.readers[k] = []
        o.signal = False
        o.waits = []
        self.ops.append(o)
        return o

    def dma(self, q, out, in_, reads=(), writes=(), **kw):
        def fn(eng):
            return eng.dma_start(out=out, in_=in_, **kw)
        return self.op(q, fn, reads, writes, dma=True)

    def barrier(self, scratch):
        o = Op()
        o.eng, o.dma = "dve", False
        o.fn = lambda e: e.memset(scratch, 0.0)
        o.reads, o.writes = (), ("__phase__",)
        o.idx = len(self.ops)
        o.deps = set(range(getattr(self, "last_barrier", 0), o.idx))
        self.last_barrier = o.idx
        o.signal = False
        o.waits = []
        self.ops.append(o)
        self.last_writer = {"__phase__": o.idx}
        self.readers = {}
        self.phase_key = True
        return o

    def finalize(self):
        ops = self.ops
        streams = {e: [] for e in self.ENGS}
        for o in ops:
            o.pos = len(streams[o.eng])
            streams[o.eng].append(o)
        dcount = {e: 0 for e in self.ENGS}
        for o in ops:
            if o.dma:
                i = dcount[o.eng]
                dcount[o.eng] += 1
                o.dma_slot = i % NDMA_SEM
                o.dma_val = 16 * (i // NDMA_SEM + 1)
        waited = {f: {e: -1 for e in self.ENGS} for f in self.ENGS}
        dma_waited = {f: {} for f in self.ENGS}
        for o in ops:
            F = o.eng
            need = {}
            dneed = {}
            rset = set(o.reads)
            for d in o.deps:
                A = ops[d]
                if A.dma:
                    key = (A.eng, A.dma_slot)
                    if dneed.get(key, 0) < A.dma_val:
                        dneed[key] = A.dma_val
                    continue
                if A.eng == F:
                    if F == "pe":
                        continue
                    if not STRICT_SAME_ENGINE and not (set(A.writes) & rset):
                        continue
                if need.get(A.eng, -1) < A.pos:
                    need[A.eng] = A.pos
            for E, p in need.items():
                if waited[F][E] >= p:
                    continue
                waited[F][E] = p
                A = streams[E][p]
                if A.dma:
                    q = p
                    while q >= 0 and streams[E][q].dma:
                        q -= 1
                    if q < 0:
                        continue
                    A = streams[E][q]
                A.signal = True
                o.waits.append(("eng", A, 0))
            for key, val in dneed.items():
                if dma_waited[F].get(key, 0) >= val:
                    continue
                dma_waited[F][key] = val
                o.waits.append(("dma", key, val))
            if o.dma and o.dma_val > 16:
                key = (o.eng, o.dma_slot)
                if dma_waited[F].get(key, 0) < o.dma_val - 16:
                    dma_waited[F][key] = o.dma_val - 16
                    o.waits.append(("dma", key, o.dma_val - 16))
        for e in self.ENGS:
            c = 0
            for o in streams[e]:
                if (not o.dma) and o.signal:
                    c += 1
                    o.semval = c
        self.streams = streams

    def emit(self, st):
        nc = self.nc
        esem = {e: st.enter_context(nc.semaphore("s_" + e)) for e in self.ENGS}
        dsem = {e: [st.enter_context(nc.semaphore("d_%s%d" % (e, i))) for i in range(NDMA_SEM)]
                for e in self.ENGS}
        self.finalize()
        block = st.enter_context(nc.Block())
        streams = self.streams

        def run(eng, e):
            for o in streams[e]:
                for kind, A, val in o.waits:
                    if kind == "eng":
                        eng.wait_ge(esem[A.eng], A.semval)
                    else:
                        eng.wait_ge(dsem[A[0]][A[1]], val)
                ins = o.fn(eng)
                if o.dma:
                    ins.then_inc(dsem[e][o.dma_slot], 16)
                elif o.signal:
                    ins.then_inc(esem[e], 1)
            cnt = sum(1 for o in streams[e] if o.dma)
            for s in range(NDMA_SEM):
                n = (cnt - s + NDMA_SEM - 1) // NDMA_SEM if cnt > s else 0
                if n > 0:
                    eng.wait_ge(dsem[e][s], 16 * n)

        @block.tensor
        def _(eng):
            run(eng, "pe")

        @block.scalar
        def _(eng):
            run(eng, "act")

        @block.vector
        def _(eng):
            run(eng, "dve")

        @block.gpsimd
        def _(eng):
            run(eng, "pool")

        @block.sync
        def _(eng):
            run(eng, "sp")


class Ctx:
    def __init__(self, nc):
        self.nc = nc
        self.S = Sched(nc)
        self.off = SB_BASE
        self.base = SB_BASE
        self.nalloc = 0
        self.dram = {}

    def sb(self, name, shape, dt):
        n = 1
        for s in shape[1:]:
            n *= s
        size = n * (4 if dt == F32 else 2)
        size = (size + 63) // 64 * 64
        assert self.off + size <= SB_LIMIT, (name, self.off, size)
        key = (name, self.off, tuple(shape), str(dt))
        cache = self.__dict__.setdefault("_sbcache", {})
        t = cache.get(key)
        if t is None:
            t = self.nc.alloc_sbuf_tensor_at("%s_%d" % (name, self.nalloc), list(shape), dt, offset=self.off)
            cache[key] = t
            self.nalloc += 1
        self.off += size
        return t

    def mark(self):
        return self.off

    def release(self, m):
        self.off = m

    def barrier(self):
        self.S.barrier(self.scr[:, 0:1])


CB_IDENT, CB_ONES, CB_TRIU, CB_TRILI, CB_MBD, CB_MBP, CB_MO0, CB_MO1, CB_END = \
    0, 128, 256, 384, 512, 640, 768, 1024, 1280
CF_IDENT, CF_NEGM, CF_BIASM, CF_PAST, CF_RC, CF_END = 0, 128, 2176, 4224, 4736, 4800


def host_consts():
    p = np.arange(128)[:, None]
    f = np.arange(128)[None, :]
    cb = np.zeros((128, CB_END), np.float32)
    cb[:, CB_IDENT:CB_IDENT + 128] = (p == f)
    cb[:, CB_ONES:CB_ONES + 128] = 1.0
    cb[:, CB_TRIU:CB_TRIU + 128] = (p > f)
    cb[:, CB_TRILI:CB_TRILI + 128] = (p <= f)
    cb[:, CB_MBD:CB_MBD + 128] = np.where(p <= f, 0.0, NEG)
    cb[:, CB_MBP:CB_MBP + 128] = np.where(p >= f, 0.0, NEG)
    q = np.arange(256)[None, :]
    cb[:, CB_MO0:CB_MO0 + 256] = np.where(p <= q, 0.0, NEG)
    cb[:, CB_MO1:CB_MO1 + 256] = np.where(128 + p <= q, 0.0, NEG)
    cf = np.zeros((128, CF_END), np.float32)
    cf[:, CF_IDENT:CF_IDENT + 128] = (p == f)
    q5 = np.arange(512)[None, :]
    for c in range(4):
        allowed = (c * 128 + p) < q5
        cf[:, CF_NEGM + c * 512:CF_NEGM + (c + 1) * 512] = np.where(allowed, -1.0, 0.0)
        cf[:, CF_BIASM + c * 512:CF_BIASM + (c + 1) * 512] = np.where(allowed, 0.0, NEG)
    past = np.zeros((32, 16), np.float32)
    for i in range(32):
        ob = i // 2
        past[i, ob:] = -1e30
    cf[:, CF_PAST:CF_PAST + 512] = past.reshape(1, 512)
    rc = np.zeros((4, 16), np.float32)
    for g, w in enumerate((2, 4, 8, 16)):
        t = np.arange(16)
        rc[g] = 1.0 / np.minimum(t + 1, w)
    cf[:, CF_RC:CF_RC + 64] = rc.reshape(1, 64)
    inv = 1.0 / (10000.0 ** (np.arange(0, 128, 2, dtype=np.float32) / 128))
    ang = np.arange(SEQ, dtype=np.float32)[:, None] * inv[None, :]
    cos = np.cos(ang).astype(np.float32).T
    sin = np.sin(ang).astype(np.float32).T
    cos2 = np.concatenate([cos, cos], 0)
    sinx = np.concatenate([sin, -sin], 0)
    return (cb.astype(ml_dtypes.bfloat16), cf, np.ascontiguousarray(cos2), np.ascontiguousarray(sinx))


def setup_common(K, cb_d, cf_d):
    nc, S = K.nc, K.S
    K.scr = K.sb("scr", [128, 16], F32)
    K.cb = K.sb("cb", [128, CB_END], BF16)
    K.cfi = K.sb("cfi", [128, 128], F32)
    S.dma("sp", K.cb[:], cb_d, writes=["cb"])
    S.dma("sp", K.cfi[:], cf_d[:, CF_IDENT:CF_IDENT + 128], writes=["cfi"])
    K.ident = K.cb[:, CB_IDENT:CB_IDENT + 128]
    K.ones = K.cb[:, CB_ONES:CB_ONES + 128]
    K.base = K.off
    K.pg = [nc.alloc_psum_tensor("pg%d" % i, [128, 2, 512], F32) for i in range(3)]
    K.pm = nc.alloc_psum_tensor("pm", [128, 512], F32)
    K.pm2 = nc.alloc_psum_tensor("pm2", [128, 512], F32)
    K.pgi = 0


def next_pg(K):
    i = K.pgi % 3
    K.pgi += 1
    return i


def load_gain_cols(K, gains_d, layer, j, name):
    t = K.sb(name, [128, 16], F32)
    K.S.dma("sp", t[:], gains_d[layer, j].rearrange("(k p) -> p k", p=128), writes=[name],
            allow_slow_non_contiguous=True)
    return t


def rstd_from_stats(K, pst_ap, rstd_ap, rkeys, wkey, n):
    S = K.S
    S.op("act", lambda e: e.activation(out=rstd_ap, in_=pst_ap, func=AF.Sqrt, scale=1.0 / D, bias=EPS),
         reads=rkeys, writes=[wkey])
    S.op("dve", lambda e: e.reciprocal(out=rstd_ap, in_=rstd_ap), reads=[wkey], writes=[wkey])


def prenorm(K, name, nblk, get_xblk, gcol, hnT):
    S = K.S
    m = K.mark()
    sq = [K.sb("sq", [128, 16, 512], BF16) for _ in range(2)]
    rstd = [K.sb("rstd", [128, 512], F32) for _ in range(2)]
    for b in range(nblk):
        xb, xkeys = get_xblk(b)
        s = b % 2
        for q4 in range(4):
            S.op("act", lambda e, q4=q4, xb=xb, s=s: e.activation(out=sq[s][:, q4 * 4:(q4 + 1) * 4, :],
                                                                   in_=xb[:, q4 * 4:(q4 + 1) * 4, :], func=AF.Square),
                 reads=xkeys, writes=[(name, "sq", s, q4)])

        def mm(e, s=s):
            r = None
            for k in range(16):
                r = e.matmul(K.pm[:], lhsT=K.ones, rhs=sq[s][:, k, :], start=(k == 0), stop=(k == 15))
            return r
        S.op("pe", mm, reads=["cb"] + [(name, "sq", s, q4) for q4 in range(4)], writes=["pm"])
        rstd_from_stats(K, K.pm[:], rstd[s][:], ["pm"], (name, "rstd", s), 512)
        for k in range(16):
            S.op("dve", lambda e, k=k, xb=xb, s=s, b=b: e.scalar_tensor_tensor(
                out=hnT[:, k, b * 512:(b + 1) * 512], in0=xb[:, k, :], scalar=gcol[:, k:k + 1],
                in1=rstd[s][:], op0=ALU.mult, op1=ALU.mult),
                reads=list(xkeys) + [(name, "rstd", s), "gcol"], writes=[(name, "hn", b)])
    K.release(m)


def dense_A(K, name, w_d, col0, nchunks, kc, rhs_of, rkeys_of, npairs, evac, wslots, slab_cols=256):
    S = K.S
    cps = slab_cols // 128
    nslab = (nchunks + cps - 1) // cps
    for sl in range(nslab):
        ws = wslots[K.wsi % len(wslots)]
        wkey = ("wslot", id(ws))
        K.wsi += 1
        c0 = col0 + sl * slab_cols
        ncols = min(slab_cols, (nchunks - sl * cps) * 128)
        S.dma("pool", ws[:, :, 0:ncols], w_d[:, c0:c0 + ncols].rearrange("(k p) c -> p k c", p=128),
              writes=[wkey])
        for cc in range(ncols // 128):
            c = sl * cps + cc
            for pr in range(npairs):
                gi = next_pg(K)
                pg = K.pg[gi]
                pkey = ("pg", gi)

                def mm(e, ws=ws, cc=cc, pr=pr, pg=pg):
                    r = None
                    for k in range(kc):
                        for b2 in range(2):
                            r = e.matmul(pg[:, b2, :], lhsT=ws[:, k, cc * 128:(cc + 1) * 128],
                                         rhs=rhs_of(k, pr * 2 + b2), start=(k == 0), stop=(k == kc - 1))
                    return r
                S.op("pe", mm, reads=[wkey] + rkeys_of(pr), writes=[pkey])
                evac(c, pr, pg, pkey)


def phase_inproj(K, layer, x_d, xT_d, gains_d, w_d, cos_d, sin_d, qT_d, kT_d, v_d, catT_d, poolw_d, pscale_d, cf_d,
                 sel_d=None, xM_d=None):
    nc, S = K.nc, K.S
    K.off = K.base
    K.wsi = 0
    gcol = load_gain_cols(K, gains_d, layer, 0, "gcol")
    hnT = K.sb("hnT", [128, 16, 2048], BF16)
    cos2 = K.sb("cos2", [128, 2048], F32)
    sinx = K.sb("sinx", [128, 2048], F32)
    if layer == 0:
        poolw = K.sb("poolw", [128, 4, 2, 256], BF16)
        S.dma("pool", poolw[:], poolw_d.rearrange("g (c p) d -> p g c d", p=128), writes=["poolw"])
        psc = K.sb("psc", [128, 8], F32)
        S.dma("sp", psc[:], pscale_d.rearrange("(k p) -> p k", p=128), writes=["psc"], allow_slow_non_contiguous=True)
        rcf = K.sb("rcf", [128, 4, 16], F32)
        S.dma("sp", rcf[:], cf_d[:, CF_RC:CF_RC + 64].rearrange("p (g t) -> p g t", g=4), writes=["rcf"])
        halo = K.sb("halo", [128, 8, 16], F32)
        S.op("dve", lambda e: e.memset(halo[:], 0.0), writes=[("halo", c) for c in range(8)])
    else:
        selt = K.sb("selt", [128, 2], F32)
        S.dma("sp", selt[:], sel_d, writes=["selt"])
    persist = K.mark()
    for sb in range(T // 2048):
        t0 = sb * 2048
        K.barrier()
        K.release(persist)
        S.dma("sp", cos2[:], cos_d[:, t0:t0 + 2048], writes=["cos2"])
        S.dma("sp", sinx[:], sin_d[:, t0:t0 + 2048], writes=["sinx"])
        m1 = K.mark()
        xTb = [K.sb("xTb", [128, 16, 512], F32) for _ in range(2)]
        if layer == 0:
            xt = [K.sb("xt", [128, 2048], F32) for _ in range(2)]
        elif sb == 1:
            xhi = [K.sb("xhi", [128, 4, 512], F32) for _ in range(2)]

        def get_xblk(b):
            s = b % 2
            tb = t0 + b * 512
            key = ("xTb", s)
            if layer == 0:
                for tt in range(4):
                    u = (b * 4 + tt) % 2
                    S.dma("sp", xt[u][:], x_d[tb + tt * 128: tb + (tt + 1) * 128, :], writes=[("xt", u)])
                    for kg in range(4):
                        def tp(e, u=u, kg=kg):
                            r = None
                            for j in range(4):
                                k = kg * 4 + j
                                r = e.transpose(out=K.pm[:, j * 128:(j + 1) * 128], in_=xt[u][:, k * 128:(k + 1) * 128],
                                                identity=K.cfi[:])
                            return r
                        S.op("pe", tp, reads=[("xt", u), "cfi"], writes=["pm"])
                        S.op("act", lambda e, s=s, kg=kg, tt=tt: e.activation(
                            out=xTb[s][:, kg * 4:(kg + 1) * 4, tt * 128:(tt + 1) * 128],
                            in_=K.pm[:].rearrange("p (j t) -> p j t", j=4), func=AF.Copy),
                            reads=["pm"], writes=[key])
                S.dma("sp", xT_d[:, :, tb:tb + 512].rearrange("k p t -> p k t"), xTb[s][:], reads=[key],
                      writes=[("xT_d", tb)])
            elif sb == 0:
                S.dma("sp", xTb[s][:], xT_d[:, :, tb:tb + 512].rearrange("k p t -> p k t"), writes=[key])
            else:
                lo0 = b * 512
                S.dma("sp", xTb[s][:], xT_d[:, :, lo0:lo0 + 512].rearrange("k p t -> p k t"), writes=[key])
                for kg in range(4):
                    u = (b * 4 + kg) % 2
                    S.dma("sp", xhi[u][:], xT_d[kg * 4:(kg + 1) * 4, :, 2048 + lo0:2048 + lo0 + 512].rearrange("k p t -> p k t"),
                          writes=[("xhi", u)])
                    S.op("dve", lambda e, u=u: e.tensor_scalar(out=xhi[u][:], in0=xhi[u][:], scalar1=selt[:, 1:2],
                                                               scalar2=None, op0=ALU.mult),
                         reads=[("xhi", u), "selt"], writes=[("xhi", u)])
                    S.op("dve", lambda e, u=u, s=s, kg=kg: e.scalar_tensor_tensor(
                        out=xTb[s][:, kg * 4:(kg + 1) * 4, :], in0=xTb[s][:, kg * 4:(kg + 1) * 4, :], scalar=selt[:, 0:1],
                        in1=xhi[u][:], op0=ALU.mult, op1=ALU.add), reads=[key, ("xhi", u), "selt"], writes=[key])
                S.dma("sp", xM_d[:, :, lo0:lo0 + 512].rearrange("k p t -> p k t"), xTb[s][:], reads=[key],
                      writes=[("xM_d", lo0)])
            return xTb[s], [key]
        K.gcol = gcol
        prenorm(K, "pnA", 4, get_xblk, gcol, hnT)
        K.barrier()
        K.release(m1)
        wslots = [K.sb("wsl", [128, 16, 256], BF16) for _ in range(3)]
        rhs_of = lambda k, blk: hnT[:, k, blk * 512:(blk + 1) * 512]
        rkeys_of = lambda pr: [("pnA", "hn", pr * 2), ("pnA", "hn", pr * 2 + 1)]
        if layer == 0:
            ev_pool(K, w_d, rhs_of, rkeys_of, wslots, poolw, psc, rcf, halo, catT_d, t0, sb)
            nh = 24
            qcol, kcol, vcol, nv = 1024, 1024 + 3072, 1024 + 6144, 3072
        else:
            nh = 16
            qcol, kcol, vcol, nv = 0, 0, 0, 0
        t1 = [K.sb("t1", [128, 1024], F32) for _ in range(2)]
        t2 = [K.sb("t2", [128, 1024], F32) for _ in range(2)]
        hr = [K.sb("hr", [128, 2048], BF16) for _ in range(3)]
        K.hri = 0
        for which, dst_d in (("q", qT_d), ("k", kT_d)):
            if layer == 1 and which == "q" and sb == 0:
                continue
            for h in range(nh):
                if layer == 0:
                    g = h // 8
                    d = DILS[g]
                    col = (qcol if which == "q" else kcol) + h * 128
                    rope = True
                else:
                    d = 1
                    if h < 8:
                        col = (0 if which == "q" else 1024) + h * 128
                        rope = False
                    else:
                        col = (3072 if which == "q" else 4096) + (h - 8) * 128
                        rope = True
                hs = K.hri % 3
                K.hri += 1
                hrt = hr[hs]
                hkey = ("hr", hs)
                L2 = 2048 // d
                mloc = 1024 // d

                def evac(c, pr, pg, pkey, d=d, rope=rope, hrt=hrt, hkey=hkey, mloc=mloc):
                    s = K.evi % 2
                    K.evi += 1
                    pflat = pg[:].rearrange("p b t -> p (b t)")
                    dst = hrt[:].rearrange("p (r n) -> p r n", r=d)[:, :, pr * mloc:(pr + 1) * mloc]
                    if rope:
                        cs = cos2[:, pr * 1024:(pr + 1) * 1024]
                        S.op("dve", lambda e: e.tensor_tensor(out=t1[s][:], in0=pflat, in1=cs, op=ALU.mult),
                             reads=[pkey, "cos2"], writes=[("t1", s)])
                        S.op("dve", lambda e: e.tensor_tensor(out=t2[s][0:64, :], in0=pflat[64:128, :],
                                                              in1=sinx[64:128, pr * 1024:(pr + 1) * 1024], op=ALU.mult),
                             reads=[pkey, "sinx"], writes=[("t2", s, 0)])
                        S.op("dve", lambda e: e.tensor_tensor(out=t2[s][64:128, :], in0=pflat[0:64, :],
                                                              in1=sinx[0:64, pr * 1024:(pr + 1) * 1024], op=ALU.mult),
                             reads=[pkey, "sinx"], writes=[("t2", s, 1)])
                        S.op("dve", lambda e: e.tensor_tensor(
                            out=dst, in0=t1[s][:].rearrange("p (m r) -> p r m", r=d),
                            in1=t2[s][:].rearrange("p (m r) -> p r m", r=d), op=ALU.add),
                            reads=[("t1", s), ("t2", s, 0), ("t2", s, 1)], writes=[hkey])
                    else:
                        S.op("act", lambda e: e.activation(out=dst, in_=pflat.rearrange("p (m r) -> p r m", r=d),
                                                           func=AF.Copy), reads=[pkey], writes=[hkey])
                K.evi = getattr(K, "evi", 0)
                dense_A(K, "proj", w_d, col, 1, 16, rhs_of, rkeys_of, 2, evac, wslots, slab_cols=128)
                Lfull = SEQ // d
                sbq = 0 if (layer == 1 and which == "q") else sb
                dstd = dst_d[h].rearrange("p (r n) -> p r n", r=d)[:, :, sbq * L2:(sbq + 1) * L2]
                S.dma("sp", dstd, hrt[:].rearrange("p (r n) -> p r n", r=d), reads=[hkey],
                      writes=[(which + "T_d", h, sb)])
        if layer == 0:
            vranges = [(1024 + 6144, 3072, 0)]
        else:
            vranges = [(2048, 1024, 0), (5120, 1024, 1024)]
        vst = [K.sb("vst", [128, 4, 256], BF16) for _ in range(2)]
        K.vsi = 0
        for (vc0, vn, vdst0) in vranges:
            for sl in range(vn // 256):
                ws = wslots[K.wsi % 3]
                wkey = ("wslot", id(ws))
                K.wsi += 1
                c0 = vc0 + sl * 256
                S.dma("pool", ws[:, :, 0:256], w_d[:, c0:c0 + 256].rearrange("(k p) c -> p k c", p=128),
                      writes=[wkey])
                for t4 in range(4):
                    gi = next_pg(K)
                    pg = K.pg[gi]
                    pkey = ("pg", gi)
                    pv = pg[:].rearrange("p b (u c) -> p (b u) c", u=2)

                    def mm(e, ws=ws, t4=t4, pv=pv):
                        r = None
                        for tt in range(4):
                            tok = (t4 * 4 + tt) * 128
                            for k in range(16):
                                r = e.matmul(pv[:, tt, :], lhsT=hnT[:, k, tok:tok + 128], rhs=ws[:, k, 0:256],
                                             start=(k == 0), stop=(k == 15))
                        return r
                    S.op("pe", mm, reads=[wkey] + [("pnA", "hn", b) for b in range(4)], writes=[pkey])
                    vs = K.vsi % 2
                    K.vsi += 1
                    S.op("act", lambda e, vs=vs, pv=pv: e.activation(out=vst[vs][:], in_=pv, func=AF.Copy),
                         reads=[pkey], writes=[("vst", vs)])
                    r0 = t0 + t4 * 512
                    cd = vdst0 + sl * 256
                    S.dma("sp", v_d[r0:r0 + 512, cd:cd + 256].rearrange("(t p) c -> p t c", p=128), vst[vs][:],
                          reads=[("vst", vs)], writes=[("v_d", r0, cd)])
    K.barrier()


def ev_pool(K, w_d, rhs_of, rkeys_of, wslots, poolw, psc, rcf, halo, catT_d, t0, sb):
    S = K.S
    U = [K.sb("U", [128, 16 + 2048], F32) for _ in range(2)]
    A1 = K.sb("A1", [128, 16 + 2048], F32)
    A2 = K.sb("A2", [128, 16 + 2048], F32)
    pooled = [K.sb("pooled", [128, 2048], BF16) for _ in range(2)]
    ast = [K.sb("ast", [128, 2048], BF16) for _ in range(2)]
    tmp16 = K.sb("tmp16", [128, 16], F32)
    for g in range(4):
        w = 2 << g
        nst = g + 1
        for cc in range(2):
            c = g * 2 + cc
            Ut = U[cc]
            ukey = ("U", cc)
            S.op("dve", lambda e, Ut=Ut, c=c: e.tensor_copy(out=Ut[:, 0:16], in_=halo[:, c, :]),
                 reads=[("halo", c)], writes=[(ukey, "h")])

            def evac(c_, pr, pg, pkey, Ut=Ut, ukey=ukey):
                S.op("act", lambda e: e.activation(out=Ut[:, 16 + pr * 1024:16 + (pr + 1) * 1024],
                                                   in_=pg[:].rearrange("p b t -> p (b t)"), func=AF.Copy),
                     reads=[pkey], writes=[(ukey, pr)])
            dense_A(K, "poolproj", w_d, c * 128, 1, 16, rhs_of, rkeys_of, 2, evac, wslots, slab_cols=128)
            ukeys = [(ukey, "h"), (ukey, 0), (ukey, 1)]
            S.op("dve", lambda e, Ut=Ut, c=c: e.tensor_copy(out=halo[:, c, :], in_=Ut[:, 2048:2064]),
                 reads=ukeys, writes=[("halo", c)])
            src, skeys = Ut, ukeys
            lo = 0
            for st in range(nst):
                sh = 1 << st
                lo = lo + sh
                dstb = A1 if st % 2 == 0 else A2
                dk = "A1" if st % 2 == 0 else "A2"
                S.op("dve", lambda e, src=src, dstb=dstb, lo=lo, sh=sh: e.tensor_tensor(
                    out=dstb[:, lo:2064], in0=src[:, lo:2064], in1=src[:, lo - sh:2064 - sh], op=ALU.add),
                    reads=skeys, writes=[dk])
                src, skeys = dstb, [dk]
            pk = ("pooled", cc)
            S.op("dve", lambda e, src=src, Ut=Ut, cc=cc, w=w: e.scalar_tensor_tensor(
                out=pooled[cc][:], in0=src[:, 16:2064], scalar=1.0 / w, in1=Ut[:, 16:2064],
                op0=ALU.mult, op1=ALU.subtract), reads=skeys + ukeys, writes=[pk])
            if sb == 0:
                S.op("dve", lambda e, src=src, g=g: e.tensor_tensor(out=tmp16[:], in0=src[:, 16:32], in1=rcf[:, g, :],
                                                                    op=ALU.mult), reads=skeys + ["rcf"], writes=["tmp16"])
                S.op("dve", lambda e, Ut=Ut, cc=cc: e.tensor_tensor(out=pooled[cc][:, 0:16], in0=tmp16[:],
                                                                     in1=Ut[:, 16:32], op=ALU.subtract),
                     reads=["tmp16"] + ukeys + [pk], writes=[pk])
        for dc in range(2):
            oc = g * 2 + dc
            for pr in range(2):
                gi = next_pg(K)
                pg = K.pg[gi]
                pkey = ("pg", gi)

                def mm(e, g=g, dc=dc, pr=pr, pg=pg):
                    r = None
                    for cc in range(2):
                        for b2 in range(2):
                            blk = pr * 2 + b2
                            r = e.matmul(pg[:, b2, :], lhsT=poolw[:, g, cc, dc * 128:(dc + 1) * 128],
                                         rhs=pooled[cc][:, blk * 512:(blk + 1) * 512], start=(cc == 0), stop=(cc == 1))
                    return r
                S.op("pe", mm, reads=["poolw", ("pooled", 0), ("pooled", 1)], writes=[pkey])
                S.op("act", lambda e, dc=dc, pr=pr, pg=pg, oc=oc: e.activation(
                    out=ast[dc][:, pr * 1024:(pr + 1) * 1024], in_=pg[:].rearrange("p b t -> p (b t)"),
                    func=AF.Copy, scale=psc[:, oc:oc + 1]), reads=[pkey, "psc"], writes=[("ast", dc, pr)])
            S.dma("sp", catT_d[oc, :, t0:t0 + 2048], ast[dc][:], reads=[("ast", dc, 0), ("ast", dc, 1)],
                  writes=[("catT_d", oc, sb)])


def phase_dilated(K, qT_d, kT_d, v_d, catT_d):
    S = K.S
    K.off = K.base
    K.barrier()
    mbd = K.cb[:, CB_MBD:CB_MBD + 128]
    mbp = K.cb[:, CB_MBP:CB_MBP + 128]
    acc = [K.sb("acc", [128, 2, T], F32) for _ in range(1)]
    ost = [K.sb("ost", [128, T], BF16) for _ in range(2)]
    rec = K.sb("rec", [128, 1024], F32)
    qs = [K.sb("qs", [128, T], BF16) for _ in range(2)]
    ks = [K.sb("ks", [128, SEQ], BF16) for _ in range(2)]
    vs = [K.sb("vs", [128, 32, 128], BF16) for _ in range(2)]
    pt = [K.sb("pt", [128, 2, 128], BF16) for _ in range(3)]
    li = 0
    K.dui = 0
    for h in range(8):
        a_ = acc[0]
        for g in range(3):
            d = DILS[g]
            L = SEQ // d
            npt = L // 128
            hh = g * 8 + h
            s = li % 2
            li += 1
            S.dma("sp", qs[s][:], qT_d[hh], writes=[("qs", s)])
            S.dma("sp", ks[s][:], kT_d[hh], writes=[("ks", s)])
            for r in range(d):
                src = v_d[:, hh * 128:(hh + 1) * 128].rearrange("(a i r) c -> r i a c", i=128, r=d)[r]
                S.dma("sp", vs[s][:, r * npt:(r + 1) * npt, :], src, writes=[("vs", s, r)])
            vkeys = [("vs", s, r) for r in range(d)]
            def stA(j, s=s, npt=npt, d=d, g=g):
                a = j % npt
                r = j // npt
                n0 = a * 128
                pgi = next_pg(K)
                pg = K.pg[pgi]
                pkey = ("pg", pgi)
                ps = pg[:, 0, 0:256].rearrange("p (c q) -> p c q", c=2)
                po = pg[:, 1, 0:256].rearrange("p (c q) -> p c q", c=2)
                kts = ([(j - 1, 0, mbp)] if a > 0 else []) + [(j, 1, mbd)]

                def mm1(e):
                    r_ = None
                    for (kt, slot, mb) in kts:
                        e.matmul(ps[:, slot, :], lhsT=ks[s][:, kt * 128:(kt + 1) * 128],
                                 rhs=qs[s][:, j * 128:(j + 1) * 128], start=True, stop=False)
                        r_ = e.matmul(ps[:, slot, :], lhsT=K.ident, rhs=mb, start=False, stop=True)
                    return r_
                S.op("pe", mm1, reads=[("qs", s), ("ks", s), "cb"], writes=[(pkey, "s")])
                u = K.dui % 3
                K.dui += 1
                lo = 0 if a > 0 else 1
                S.op("act", lambda e: e.activation(out=pt[u][:, lo:2, :], in_=ps[:, lo:2, :], func=AF.Exp, scale=SCALE),
                     reads=[(pkey, "s")], writes=[("pt", u)])
                return (j, a, r, n0, pkey, po, kts, u)

            def stB(info, s=s, d=d, g=g, vkeys=vkeys):
                j, a, r, n0, pkey, po, kts, u = info

                def mm2(e):
                    r_ = None
                    n = len(kts)
                    for i, (kt, slot, mb) in enumerate(kts):
                        e.matmul(po[:, 0, :], lhsT=vs[s][:, kt, :], rhs=pt[u][:, slot, :], start=(i == 0), stop=(i == n - 1))
                    for i, (kt, slot, mb) in enumerate(kts):
                        r_ = e.matmul(po[:, 1, :], lhsT=K.ones, rhs=pt[u][:, slot, :], start=(i == 0), stop=(i == n - 1))
                    return r_
                S.op("pe", mm2, reads=[("pt", u), "cb"] + vkeys, writes=[(pkey, "o")])
                dst = a_[:].rearrange("p c (n r) -> p c n r", r=d)[:, :, n0:n0 + 128, r]
                t_lo = (n0 * d) // 128
                t_hi = ((n0 + 127) * d + r) // 128
                akeys = [("acc", t) for t in range(t_lo, t_hi + 1)]
                if g == 0:
                    S.op("act", lambda e: e.activation(out=dst, in_=po, func=AF.Copy), reads=[(pkey, "o")], writes=akeys)
                else:
                    S.op("dve", lambda e: e.tensor_tensor(out=dst, in0=po, in1=dst, op=ALU.add),
                         reads=[(pkey, "o")] + akeys, writes=akeys)

            pend = None
            for j in range(32):
                info = stA(j)
                if pend is not None:
                    stB(pend)
                pend = info
            stB(pend)
        os_ = h % 2
        for q4 in range(4):
            sl = slice(q4 * 1024, (q4 + 1) * 1024)
            akeys = [("acc", t) for t in range(q4 * 8, q4 * 8 + 8)]
            S.op("dve", lambda e, sl=sl: e.reciprocal(out=rec[:], in_=a_[:, 1, sl]), reads=akeys, writes=["rec"])
            S.op("dve", lambda e, sl=sl, os_=os_: e.tensor_tensor(out=ost[os_][:, sl], in0=a_[:, 0, sl], in1=rec[:],
                                                                  op=ALU.mult), reads=akeys + ["rec"],
                 writes=[("ost", os_, q4)])
        S.dma("sp", catT_d[h], ost[os_][:], reads=[("ost", os_, q4) for q4 in range(4)],
              writes=[("catT_d", 8 + h)])
    K.barrier()


def phase_sb(K, qT_d, kT_d, v_d, catT_d, cf_d, pb_d):
    S = K.S
    K.off = K.base
    K.barrier()
    triu = K.cb[:, CB_TRIU:CB_TRIU + 128]
    trili = K.cb[:, CB_TRILI:CB_TRILI + 128]
    negm = K.sb("negm", [128, 4, 512], F32)
    biasm = K.sb("biasm", [128, 4, 512], F32)
    S.dma("sp", negm[:], cf_d[:, CF_NEGM:CF_NEGM + 2048].rearrange("p (c q) -> p c q", c=4), writes=["negm"])
    S.dma("sp", biasm[:], cf_d[:, CF_BIASM:CF_BIASM + 2048].rearrange("p (c q) -> p c q", c=4), writes=["biasm"])
    TQ = 2048
    prevb = K.sb("prevb", [128, 1], F32)
    S.dma("sp", prevb[:], pb_d, writes=["prevb"])
    qs = [K.sb("qs", [128, TQ], BF16) for _ in range(2)]
    ks = [K.sb("ks", [128, SEQ], BF16) for _ in range(2)]
    vs = [K.sb("vs", [128, 32, 128], BF16) for _ in range(2)]
    cst = [K.sb("cst", [128, TQ], BF16) for _ in range(2)]
    NBUF = 5
    mk = lambda name, dt: [[K.sb(name, [128, 512], dt) for _ in range(NBUF)] for _ in range(2)]
    e_sb, sp_sb, lb, ls, wb, at = mk("e_sb", F32), mk("sp_sb", F32), mk("lb", BF16), mk("ls", F32), mk("wb", F32), mk("at", BF16)
    pz = [K.pg[0][:, 0, :], K.pg[0][:, 1, :], K.pg[1][:, 0, :], K.pg[1][:, 1, :]]
    pacc = [K.pg[2][:, 0, :], K.pg[2][:, 1, :]]
    pos = [K.pm[:], K.pm2[:]]
    NBLK = TQ // 512
    streams = [[], []]
    for h in range(8):
        st = h % 2
        for b in range(NBLK):
            tiles = list(range(16 + 4 * b + 3, -1, -1))
            for i, kt in enumerate(tiles):
                j = len(streams[st])
                streams[st].append(dict(h=h, s=st, b=b, kt=kt, c=(kt - 16 - 4 * b if kt >= 16 else -1), prev=(kt < 16),
                                        first=(i == 0), last=(i == len(tiles) - 1), u=j % NBUF, z=(2 * j + st) % 4))

    def load_head(h):
        s = h % 2
        S.dma("sp", qs[s][:], qT_d[h], writes=[("qs", s)])
        S.dma("sp", ks[s][:], kT_d[h], writes=[("ks", s)])
        S.dma("sp", vs[s][:], v_d[:, h * 128:(h + 1) * 128].rearrange("(a i) c -> i a c", i=128), writes=[("vs", s)])

    def stage1(U):
        s, b, kt, c, u, z = U["s"], U["b"], U["kt"], U["c"], U["u"], U["z"]
        if U["first"] and b == 0:
            load_head(U["h"])
        S.op("pe", lambda e: e.matmul(pz[z], lhsT=ks[s][:, kt * 128:(kt + 1) * 128],
                                      rhs=qs[s][:, b * 512:(b + 1) * 512], start=True, stop=True),
             reads=[("qs", s), ("ks", s)], writes=[("pz", z)])
        if U["prev"]:
            S.op("act", lambda e: e.activation(out=e_sb[s][u][:], in_=pz[z], func=AF.Exp, scale=SCALE, bias=prevb[:, 0:1]),
                 reads=[("pz", z), "prevb"], writes=[("e", s, u)])
        else:
            S.op("act", lambda e: e.activation(out=e_sb[s][u][:], in_=pz[z], func=AF.Exp, scale=SCALE),
                 reads=[("pz", z)], writes=[("e", s, u)])
        S.op("act", lambda e: e.activation(out=sp_sb[s][u][:], in_=e_sb[s][u][:], func=AF.Ln, bias=1.0),
             reads=[("e", s, u)], writes=[("sp", s, u)])

    def stage1b(U):
        s, b, kt, c, u, z = U["s"], U["b"], U["kt"], U["c"], U["u"], U["z"]
        if c >= 0:
            S.op("dve", lambda e: e.tensor_tensor(out=lb[s][u][:], in0=sp_sb[s][u][:], in1=negm[:, c, :], op=ALU.mult),
                 reads=[("sp", s, u), "negm"], writes=[("lb", s, u)])
        else:
            S.op("dve", lambda e: e.tensor_scalar(out=lb[s][u][:], in0=sp_sb[s][u][:], scalar1=-1.0, scalar2=None,
                                                  op0=ALU.mult), reads=[("sp", s, u)], writes=[("lb", s, u)])
        S.op("dve", lambda e: e.scalar_tensor_tensor(out=ls[s][u][:], in0=pz[z], scalar=SCALE, in1=sp_sb[s][u][:],
                                                     op0=ALU.mult, op1=ALU.subtract),
             reads=[("pz", z), ("sp", s, u)], writes=[("ls", s, u)])
        if c >= 0:
            S.op("dve", lambda e: e.tensor_tensor(out=ls[s][u][:], in0=ls[s][u][:], in1=biasm[:, c, :], op=ALU.add),
                 reads=[("ls", s, u), "biasm"], writes=[("ls", s, u)])

    def stage2a(U):
        s, u, first = U["s"], U["u"], U["first"]
        S.op("pe", lambda e: e.matmul(pacc[s], lhsT=triu, rhs=lb[s][u][:], start=first, stop=False, skip_group_check=True),
             reads=[("lb", s, u), "cb"], writes=[("pacc", s)])
        S.op("dve", lambda e: e.tensor_tensor(out=wb[s][u][:], in0=pacc[s], in1=ls[s][u][:], op=ALU.add),
             reads=[("pacc", s), ("ls", s, u)], writes=[("wb", s, u)])

    def stage2b(U):
        s, u, last = U["s"], U["u"], U["last"]
        S.op("pe", lambda e: e.matmul(pacc[s], lhsT=trili, rhs=lb[s][u][:], start=False, stop=last, skip_group_check=True),
             reads=[("lb", s, u), "cb"], writes=[("pacc", s)])
        if U["prev"]:
            S.op("act", lambda e: e.activation(out=at[s][u][:], in_=wb[s][u][:], func=AF.Exp, bias=prevb[:, 0:1]),
                 reads=[("wb", s, u), "prevb"], writes=[("at", s, u)])
        else:
            S.op("act", lambda e: e.activation(out=at[s][u][:], in_=wb[s][u][:], func=AF.Exp),
                 reads=[("wb", s, u)], writes=[("at", s, u)])

    def stage3(U):
        s, b, kt, u, first, last = U["s"], U["b"], U["kt"], U["u"], U["first"], U["last"]
        po, pokey = pos[s], ("po", s)
        S.op("pe", lambda e: e.matmul(po, lhsT=vs[s][:, kt, :], rhs=at[s][u][:], start=first, stop=last,
                                      skip_group_check=True),
             reads=[("at", s, u), ("vs", s)], writes=[pokey])
        if last:
            S.op("act", lambda e: e.activation(out=cst[s][:, b * 512:(b + 1) * 512], in_=po, func=AF.Copy),
                 reads=[pokey], writes=[("cst", s, b)])
            if b == NBLK - 1:
                S.dma("sp", catT_d[U["h"]], cst[s][:], reads=[("cst", s, bb) for bb in range(NBLK)],
                      writes=[("catT_d", U["h"])])

    n = len(streams[0])
    assert len(streams[1]) == n
    get = lambda st, i: streams[st][i] if 0 <= i < n else None
    for i in range(n + 4):
        for st in range(2):
            U = get(st, i)
            if U:
                stage1(U)
        for st in range(2):
            U = get(st, i - 3)
            if U:
                stage2a(U)
        for st in range(2):
            U = get(st, i - 3)
            if U:
                stage2b(U)
            U = get(st, i - 4)
            if U:
                stage3(U)
        for st in range(2):
            U = get(st, i - 1)
            if U:
                stage1b(U)
    K.barrier()


def phase_moba(K, qT_d, kT_d, v_d, catT_d, cf_d, past_d):
    S = K.S
    K.off = K.base
    K.barrier()
    mo = [K.cb[:, CB_MO0:CB_MO0 + 256], K.cb[:, CB_MO1:CB_MO1 + 256]]
    TQ = 2048
    NQT = TQ // 128
    past = K.sb("past", [128, NQT, 16], F32)
    S.dma("sp", past[:], past_d.rearrange("p (i n) -> p i n", i=NQT), writes=["past"])
    qs = [K.sb("qs", [128, TQ], BF16) for _ in range(2)]
    ks = [K.sb("ks", [128, SEQ], BF16) for _ in range(2)]
    va = [K.sb("va", [128, 32, 129], BF16) for _ in range(2)]
    cst = [K.sb("cst", [128, TQ], BF16) for _ in range(2)]
    kmf = K.sb("kmf", [128, 16], F32)
    kmh = K.sb("kmh", [128, 16], BF16)
    kml = K.sb("kml", [128, 16], BF16)
    kmr = K.sb("kmr", [128, 16], F32)
    gm = K.sb("gm", [128, NQT, 16], F32)
    mx8 = K.sb("mx8", [128, NQT, 8], F32)
    thr = K.sb("thr", [128, NQT], F32)
    sel = K.sb("sel", [128, NQT, 16], F32)
    pt = [K.sb("pt", [128, 2, 256], BF16) for _ in range(3)]
    acc = [K.sb("macc", [128, 2, 129], F32) for _ in range(2)]
    rcp = K.sb("rcp", [128, 2], F32)
    ob16 = [K.sb("ob16", [128, 2, 128], F32) for _ in range(2)]
    for s in range(2):
        S.op("dve", lambda e, s=s: e.memset(va[s][:, :, 128:129], 1.0), writes=[("va1", s)])
    K.mui = 0
    gi_ = 0
    for h in range(8):
        hh = 8 + h
        s = h % 2
        S.dma("sp", qs[s][:], qT_d[hh], writes=[("qs", s)])
        S.dma("sp", ks[s][:], kT_d[hh], writes=[("ks", s)])
        S.dma("sp", va[s][:, :, 0:128], v_d[:, 1024 + h * 128:1024 + (h + 1) * 128].rearrange("(a i) c -> i a c", i=128),
              writes=[("va", s)])
        S.op("dve", lambda e, s=s: e.tensor_reduce(out=kmf[:], in_=ks[s][:].rearrange("p (n k) -> p n k", n=16),
                                                   axis=AX.X, op=ALU.add), reads=[("ks", s)], writes=["kmf"])
        S.op("dve", lambda e: e.tensor_scalar(out=kmf[:], in0=kmf[:], scalar1=1.0 / 256, scalar2=None, op0=ALU.mult),
             reads=["kmf"], writes=["kmf"])
        S.op("dve", lambda e: e.tensor_copy(out=kmh[:], in_=kmf[:]), reads=["kmf"], writes=["kmh"])
        S.op("dve", lambda e: e.tensor_copy(out=kmr[:], in_=kmh[:]), reads=["kmh"], writes=["kmr"])
        S.op("dve", lambda e: e.tensor_tensor(out=kml[:], in0=kmf[:], in1=kmr[:], op=ALU.subtract),
             reads=["kmf", "kmr"], writes=["kml"])
        pgate = K.pm[:, 0:NQT * 16].rearrange("p (i n) -> p i n", i=NQT)

        def gmm(e, s=s):
            r_ = None
            for i in range(NQT):
                e.matmul(pgate[:, i, :], lhsT=qs[s][:, i * 128:(i + 1) * 128], rhs=kmh[:], start=True, stop=False)
                r_ = e.matmul(pgate[:, i, :], lhsT=qs[s][:, i * 128:(i + 1) * 128], rhs=kml[:], start=False, stop=True)
            return r_
        S.op("pe", gmm, reads=[("qs", s), "kmh", "kml"], writes=["pm"])
        S.op("dve", lambda e: e.tensor_tensor(out=gm[:], in0=pgate, in1=past[:], op=ALU.add), reads=["pm", "past"],
             writes=["gm"])
        for i in range(NQT):
            S.op("dve", lambda e, i=i: e.max(out=mx8[:, i, :], in_=gm[:, i, :]), reads=["gm"], writes=[("mx8", i)])
        S.op("dve", lambda e: e.tensor_scalar(out=thr[:], in0=mx8[:, :, 2], scalar1=-1e29, scalar2=None, op0=ALU.max),
             reads=[("mx8", i) for i in range(NQT)], writes=["thr"])
        for i in range(NQT):
            S.op("dve", lambda e, i=i: e.tensor_scalar(out=sel[:, i, :], in0=gm[:, i, :], scalar1=thr[:, i:i + 1],
                                                       scalar2=None, op0=ALU.is_ge), reads=["gm", "thr"],
                 writes=[("sel", i)])
        for G in range(NQT // 2):
            aa = acc[gi_ % 2]
            akey = ("macc", gi_ % 2)
            gi_ += 1
            order = [8 + G] + list(range(8 + G))
            def stA(n, G=G, s=s):
                own = (n == 8 + G)
                pgi = next_pg(K)
                pg = K.pg[pgi]
                pkey = ("pg", pgi)
                ps = pg[:, 0, :].rearrange("p (c q) -> p c q", c=2)
                po = pg[:, 1, 0:258].rearrange("p (t c) -> p t c", t=2)

                def mm1(e):
                    r_ = None
                    for c in range(2):
                        kt = 2 * n + c
                        r_ = e.matmul(ps[:, c, :], lhsT=ks[s][:, kt * 128:(kt + 1) * 128],
                                      rhs=qs[s][:, G * 256:(G + 1) * 256], start=True, stop=not own)
                        if own:
                            r_ = e.matmul(ps[:, c, :], lhsT=K.ident, rhs=mo[c], start=False, stop=True)
                    return r_
                S.op("pe", mm1, reads=[("qs", s), ("ks", s), "cb"], writes=[(pkey, "s")])
                u = K.mui % 3
                K.mui += 1
                S.op("act", lambda e: e.activation(out=pt[u][:], in_=ps, func=AF.Exp, scale=SCALE),
                     reads=[(pkey, "s")], writes=[("pt", u)])
                return (n, own, pkey, po, u)

            def stB(info, G=G, s=s, aa=aa, akey=akey):
                n, own, pkey, po, u = info

                def mm2(e):
                    r_ = None
                    for t in range(2):
                        for c in range(2):
                            r_ = e.matmul(po[:, t, :], lhsT=pt[u][:, c, t * 128:(t + 1) * 128], rhs=va[s][:, 2 * n + c, :],
                                          start=(c == 0), stop=(c == 1))
                    return r_
                S.op("pe", mm2, reads=[("pt", u), ("va", s), ("va1", s)], writes=[(pkey, "o")])
                if own:
                    S.op("dve", lambda e: e.tensor_copy(out=aa[:], in_=po), reads=[(pkey, "o")],
                         writes=[(akey, 0), (akey, 1)])
                else:
                    for t in range(2):
                        S.op("dve", lambda e, t=t: e.scalar_tensor_tensor(
                            out=aa[:, t, :], in0=po[:, t, :], scalar=sel[:, 2 * G + t, n:n + 1], in1=aa[:, t, :],
                            op0=ALU.mult, op1=ALU.add),
                            reads=[(pkey, "o"), (akey, t), ("sel", 2 * G + t)], writes=[(akey, t)])

            pend = None
            for n in order:
                info = stA(n)
                if pend is not None:
                    stB(pend)
                pend = info
            stB(pend)
            S.op("dve", lambda e, aa=aa: e.reciprocal(out=rcp[:], in_=aa[:, :, 128]), reads=[(akey, 0), (akey, 1)], writes=["rcp"])
            o_ = ob16[G % 2]
            for t in range(2):
                S.op("dve", lambda e, aa=aa, t=t, o_=o_: e.tensor_scalar(out=o_[:, t, :], in0=aa[:, t, 0:128],
                                                                         scalar1=rcp[:, t:t + 1], scalar2=None,
                                                                         op0=ALU.mult),
                     reads=[(akey, t), "rcp"], writes=[("ob16", G % 2, t)])

            def tp(e, o_=o_):
                r_ = None
                for t in range(2):
                    r_ = e.transpose(out=K.pm2[:, t * 128:(t + 1) * 128], in_=o_[:, t, :], identity=K.cfi[:])
                return r_
            S.op("pe", tp, reads=[("ob16", G % 2, 0), ("ob16", G % 2, 1), "cfi"], writes=["pm2"])
            S.op("act", lambda e, G=G, s=s: e.activation(out=cst[s][:, G * 256:(G + 1) * 256], in_=K.pm2[:, 0:256],
                                                    func=AF.Copy), reads=["pm2"], writes=[("cst", s, G)])
        S.dma("sp", catT_d[hh], cst[s][:], reads=[("cst", s, G) for G in range(NQT // 2)], writes=[("catT_d", hh)])
    K.barrier()


def phase_post(K, layer, cat_of, xin_d, xmid_d, yT_d, gains_d, wout_d, wg_d, wu_d, wd_d, out_d, final, ntok=T):
    S = K.S
    nc = K.nc
    K.off = K.base
    K.wsi = 0
    K.barrier()
    g1 = load_gain_cols(K, gains_d, layer, 1, "g1")
    g2 = load_gain_cols(K, gains_d, layer, 2, "g2")
    g3 = load_gain_cols(K, gains_d, layer, 3, "g3")
    TB = 1024
    rstd = [K.sb("rstdC", [128, TB], F32) for _ in range(3)]
    hn2_off = K.off
    hn2 = K.sb("hn2", [128, 16, TB], BF16)
    h1_off = K.off
    h1 = K.sb("h1", [128, NFC, TB], BF16)
    K.nalloc += 1
    yres = nc.alloc_sbuf_tensor_at("yres_%d" % K.nalloc, [128, 16, TB], F32, offset=h1_off)
    K.nalloc += 1
    ostg = nc.alloc_sbuf_tensor_at("ostg_%d" % K.nalloc, [128, 8, 512], F32, offset=h1_off + 16 * TB * 4)
    s6 = []
    for i in range(8):
        K.nalloc += 1
        s6.append(nc.alloc_sbuf_tensor_at("s6_%d" % K.nalloc, [128, TB], F32, offset=hn2_off + i * TB * 4))
    yb, xb6 = s6[0:4], s6[4:8]
    persist = K.mark()
    pst = K.pg[2]
    pgs = [K.pg[0], K.pg[1]]
    nT = ntok // TB
    KB = 1024

    def at(off_kb, name, shape, dt):
        K.nalloc += 1
        n = 1
        for d_ in shape[1:]:
            n *= d_
        assert persist + off_kb * KB + n * (4 if dt == F32 else 2) <= SB_LIMIT, (name, off_kb)
        return nc.alloc_sbuf_tensor_at("%s_%d" % (name, K.nalloc), list(shape), dt, offset=persist + int(off_kb * KB))
    wgs = [at(0, "wgs", [128, 16, 256], BF16), at(8, "wgs", [128, 16, 256], BF16)]
    wus = [at(16, "wus", [128, 16, 256], BF16), at(24, "wus", [128, 16, 256], BF16)]
    sg = [at(32, "sg", [128, TB], F32), at(36, "sg", [128, TB], F32)]
    wds = [at(40, "wds", [128, NFC, 256], BF16), at(0, "wds", [128, NFC, 256], BF16)]
    yst5 = [at(22 + 4 * i, "yst", [128, TB], F32) for i in range(3)]
    sq5 = [at(34 + 2 * i, "sq5", [128, TB], BF16) for i in range(2)]
    wsl = [at(62, "wsl", [128, 16, 256], BF16), at(32, "wsl", [128, 16, 256], BF16)]
    cat = at(0, "cat", [128, 16, TB], BF16)
    sq1 = [at(40 + 2 * i, "sq1", [128, TB], BF16) for i in range(2)]
    xb2 = [at(44 + 4 * i, "xb2", [128, TB], F32) for i in range(4)]

    def stats_ops(c, src_flat_ap, src_keys, sqb, sqname):
        u = c % 2
        S.op("act", lambda e: e.activation(out=sqb[u][:], in_=src_flat_ap, func=AF.Square), reads=src_keys,
             writes=[(sqname, u)])

        def mm(e):
            r_ = None
            for b2 in range(2):
                r_ = e.matmul(pst[:, b2, :], lhsT=K.ones, rhs=sqb[u][:, b2 * 512:(b2 + 1) * 512], start=(c == 0),
                              stop=(c == 15), skip_group_check=True)
            return r_
        S.op("pe", mm, reads=[(sqname, u), "cb"], writes=["pst"])

    def dense_chunks(w_d, kc, rhs_of, rkeys, wslots, wname, sink, preloaded=False, after_last_dma=None):
        pgj = 0
        for sl in range(8):
            ws = wslots[sl % len(wslots)]
            wkey = (wname, sl % len(wslots))
            if not (preloaded and sl == 0):
                S.dma("pool", ws[:, 0:kc, :], w_d[:, sl * 256:(sl + 1) * 256].rearrange("(k p) c -> p k c", p=128),
                      writes=[wkey])
            if sl == 7 and after_last_dma is not None:
                after_last_dma()
            for cc in range(2):
                c = sl * 2 + cc
                gi = pgj % 2
                pgj += 1
                pg = pgs[gi]
                pkey = ("pg", gi)

                def mm(e, ws=ws, cc=cc, pg=pg):
                    r_ = None
                    for k in range(kc):
                        for b2 in range(2):
                            r_ = e.matmul(pg[:, b2, :], lhsT=ws[:, k, cc * 128:(cc + 1) * 128], rhs=rhs_of(k, b2),
                                          start=(k == 0), stop=(k == kc - 1))
                    return r_
                S.op("pe", mm, reads=[wkey] + rkeys, writes=[pkey])
                sink(c, pg, pkey)
                yield

    def gen_S1(t, cat, wsl, sqb, preloaded):
        tb0 = t * TB
        for k in range(16):
            S.dma("sp", cat[:, k, :], cat_of(k)[:, tb0:tb0 + TB], writes=[("cat", k)])

        def sink(c, pg, pkey):
            S.op("dve", lambda e: e.tensor_copy(out=yres[:, c, :], in_=pg[:].rearrange("p b t -> p (b t)")),
                 reads=[pkey], writes=[("yres", c)])
            stats_ops(c, yres[:, c, :], [("yres", c)], sqb, "sq1")
        yield from dense_chunks(wout_d, 16, lambda k, b2: cat[:, k, b2 * 512:(b2 + 1) * 512],
                                [("cat", k) for k in range(16)], wsl, "wsl", sink, preloaded=preloaded)
        rstd_from_stats(K, pst[:].rearrange("p b t -> p (b t)"), rstd[0][:], ["pst"], "r1rstd", TB)
        S.dma("pool", wgs[0][:], wg_d[:, 0:256].rearrange("(k p) c -> p k c", p=128),
              writes=[("wgs", 0)] + [("cat", k) for k in range(16)])
        S.dma("pool", wus[0][:], wu_d[:, 0:256].rearrange("(k p) c -> p k c", p=128),
              writes=[("wus", 0)] + [("cat", k) for k in range(16)])

    def emit_S2(t, xb, tbb, sqb):
        tb0 = t * TB
        for c in range(16):
            u = c % 4
            S.dma("sp", xb[u][:], xin_d[c, :, tb0:tb0 + TB], writes=[("xb2", u)])
            S.op("dve", lambda e, c=c: e.scalar_tensor_tensor(out=yres[:, c, :], in0=yres[:, c, :], scalar=g1[:, c:c + 1],
                                                              in1=rstd[0][:], op0=ALU.mult, op1=ALU.mult),
                 reads=[("yres", c), "g1", "r1rstd"], writes=[("yres", c)])
            S.op("dve", lambda e, u=u, c=c: e.tensor_tensor(out=yres[:, c, :], in0=yres[:, c, :], in1=xb[u][:], op=ALU.add),
                 reads=[("xb2", u), ("yres", c)], writes=[("yres", c)])
            stats_ops(c, yres[:, c, :], [("yres", c)], sqb, "sq1")
            S.dma("sp", xmid_d[c, :, tb0:tb0 + TB], yres[:, c, :], reads=[("yres", c)], writes=[("xmid", c)])
        rstd_from_stats(K, pst[:].rearrange("p b t -> p (b t)"), rstd[1][:], ["pst"], "r2rstd", TB)

    def emit_S3():
        for c in range(16):
            S.op("dve", lambda e, c=c: e.scalar_tensor_tensor(out=hn2[:, c, :], in0=yres[:, c, :], scalar=g2[:, c:c + 1],
                                                              in1=rstd[1][:], op0=ALU.mult, op1=ALU.mult),
                 reads=[("yres", c), "g2", "r2rstd"], writes=[("hn2", c)])

    def emit_S4():
        hkeys = [("hn2", c) for c in range(16)]
        for sl in range(FF // 256):
            wg_, wu_ = wgs[sl % 2], wus[sl % 2]
            if sl > 0:
                S.dma("pool", wg_[:], wg_d[:, sl * 256:(sl + 1) * 256].rearrange("(k p) c -> p k c", p=128),
                      writes=[("wgs", sl % 2)])
                S.dma("pool", wu_[:], wu_d[:, sl * 256:(sl + 1) * 256].rearrange("(k p) c -> p k c", p=128),
                      writes=[("wus", sl % 2)])
            if sl == FF // 256 - 1:
                S.dma("pool", wds[0][:], wd_d[:, 0:256].rearrange("(k p) c -> p k c", p=128), writes=[("wds", 0)])
            for cc in range(2):
                fc = sl * 2 + cc
                gg, gu = (2 * fc) % 3, (2 * fc + 1) % 3
                for which, wt, gi in (("g", wg_, gg), ("u", wu_, gu)):
                    pg = K.pg[gi]

                    def mm(e, wt=wt, cc=cc, pg=pg):
                        r_ = None
                        for k in range(16):
                            for b2 in range(2):
                                r_ = e.matmul(pg[:, b2, :], lhsT=wt[:, k, cc * 128:(cc + 1) * 128],
                                              rhs=hn2[:, k, b2 * 512:(b2 + 1) * 512], start=(k == 0), stop=(k == 15))
                        return r_
                    S.op("pe", mm, reads=[("wgs" if which == "g" else "wus", sl % 2)] + hkeys, writes=[("pg", gi)])
                u = fc % 2
                S.op("act", lambda e, u=u, gg=gg: e.activation(out=sg[u][:], in_=K.pg[gg][:].rearrange("p b t -> p (b t)"),
                                                               func=AF.Silu), reads=[("pg", gg)], writes=[("sg", u)])
                S.op("dve", lambda e, u=u, fc=fc, gu=gu: e.tensor_tensor(out=h1[:, fc, :], in0=sg[u][:],
                                                                         in1=K.pg[gu][:].rearrange("p b t -> p (b t)"),
                                                                         op=ALU.mult),
                     reads=[("sg", u), ("pg", gu)], writes=[("h1", fc)])

    def emit_S5(t):
        tb0 = t * TB
        yst = yst5
        sqb = sq5

        def sink(c, pg, pkey):
            ys = yst[c % 3]
            ykey = ("yst", c % 3)
            S.op("dve", lambda e: e.tensor_copy(out=ys[:], in_=pg[:].rearrange("p b t -> p (b t)")),
                 reads=[pkey], writes=[ykey])
            stats_ops(c, ys[:], [ykey], sqb, "sq5")
            S.dma("sp", yT_d[c, :, tb0:tb0 + TB], ys[:], reads=[ykey], writes=[("yT_d", c)])
        pre = None
        if t + 1 < nT:
            pre = lambda: S.dma("pool", wsl[0][:], wout_d[:, 0:256].rearrange("(k p) c -> p k c", p=128),
                                writes=[("wsl", 0)])
        for _ in dense_chunks(wd_d, NFC, lambda k, b2: h1[:, k, b2 * 512:(b2 + 1) * 512],
                              [("h1", fc) for fc in range(NFC)], wds, "wds", sink, preloaded=True, after_last_dma=pre):
            pass
        rstd_from_stats(K, pst[:].rearrange("p b t -> p (b t)"), rstd[2][:], ["pst"], "r3rstd", TB)

    def gen_S6(t):
        tb0 = t * TB
        xn6 = yb
        for c in range(16):
            u = c % 4
            S.dma("sp", yb[u][:], yT_d[c, :, tb0:tb0 + TB], writes=[("xn6", u)])
            S.dma("sp", xb6[u][:], xmid_d[c, :, tb0:tb0 + TB], writes=[("xb6", u)])
            S.op("dve", lambda e, u=u, c=c: e.scalar_tensor_tensor(out=yb[u][:], in0=yb[u][:], scalar=g3[:, c:c + 1],
                                                                   in1=rstd[2][:], op0=ALU.mult, op1=ALU.mult),
                 reads=[("xn6", u), "g3", "r3rstd"], writes=[("xn6", u)])
            S.op("dve", lambda e, u=u: e.tensor_tensor(out=yb[u][:], in0=yb[u][:], in1=xb6[u][:], op=ALU.add),
                 reads=[("xn6", u), ("xb6", u)], writes=[("xn6", u)])
            if not final:
                S.dma("sp", xmid_d[c, :, tb0:tb0 + TB], xn6[u][:], reads=[("xn6", u)], writes=[("xfin", c)])
            else:
                c4 = c % 4
                for hf in range(2):
                    def tp(e, u=u, hf=hf):
                        r_ = None
                        for j in range(4):
                            t8 = hf * 4 + j
                            r_ = e.transpose(out=K.pm[:, j * 128:(j + 1) * 128], in_=xn6[u][:, t8 * 128:(t8 + 1) * 128],
                                             identity=K.cfi[:])
                        return r_
                    S.op("pe", tp, reads=[("xn6", u), "cfi"], writes=["pm"])
                    S.op("act", lambda e, hf=hf, c4=c4: e.activation(
                        out=ostg[:, hf * 4:(hf + 1) * 4, c4 * 128:(c4 + 1) * 128],
                        in_=K.pm[:].rearrange("p (j f) -> p j f", j=4), func=AF.Copy),
                        reads=["pm"], writes=[("ostg", c4, hf)])
                if c4 == 3:
                    cg = c // 4
                    S.dma("sp", out_d[tb0:tb0 + TB, cg * 512:(cg + 1) * 512].rearrange("(t p) f -> p t f", p=128),
                          ostg[:], reads=[("ostg", i, hf) for i in range(4) for hf in range(2)],
                          writes=[("out_d", tb0, cg)])
            yield

    def stage_bufs():
        return cat, wsl, sq1, xb2, None

    K.barrier()
    cat, wsl, sqb, xb, tbb = stage_bufs()
    for _ in gen_S1(0, cat, wsl, sqb, False):
        pass
    emit_S2(0, xb, tbb, sqb)
    for t in range(nT):
        K.barrier()
        K.release(persist)
        emit_S3()
        K.barrier()
        K.release(persist)
        emit_S4()
        K.barrier()
        K.release(persist)
        emit_S5(t)
        K.barrier()
        g6 = gen_S6(t)
        if t + 1 < nT:
            cat, wsl, sqb, xb, tbb = stage_bufs()
            g1_ = gen_S1(t + 1, cat, wsl, sqb, True)
            done1 = done6 = False
            while not (done1 and done6):
                if not done1:
                    try:
                        next(g1_)
                    except StopIteration:
                        done1 = True
                if not done6:
                    try:
                        next(g6)
                    except StopIteration:
                        done6 = True
            emit_S2(t + 1, xb, tbb, sqb)
        else:
            for _ in g6:
                pass
    K.barrier()


ALL_PHASES = ["A0", "B0", "C0", "A1", "B1", "C1"]


def build_program(phases):
    from contextlib import ExitStack
    nc = bass.Bass("TRN2", target_bir_lowering=False)
    in_names = []

    def inp(name, shape, dt=F32):
        in_names.append(name)
        return nc.dram_tensor(name, shape, dt, kind="ExternalInput").ap()

    def act(name, shape, dt, producer, consumers):
        prod_in = producer in phases
        cons_in = [c for c in consumers if c in phases]
        cons_out = [c for c in consumers if c not in phases]
        if not prod_in and not cons_in:
            return None
        if not prod_in:
            return inp(name, shape, dt)
        kind = "ExternalOutput" if cons_out else "Internal"
        return nc.dram_tensor(name, shape, dt, kind=kind).ap()

    need = lambda *ps: any(p in phases for p in ps)
    cb_d = inp("cb", [128, CB_END], BF16)
    cf_d = inp("cf", [128, CF_END])
    gains_d = inp("norm_gains", [2, 4, D])
    x_d = inp("x", [T, D]) if need("A0") else None
    cos_d = inp("cos2", [128, SEQ]) if need("A0") else None
    sin_d = inp("sinx", [128, SEQ]) if need("A0") else None
    w_ab = inp("w_in_ab", [D, 10240]) if need("A0") else None
    poolw_d = inp("pool_w", [4, 256, 256]) if need("A0") else None
    psc_d = inp("pool_scale", [1024]) if need("A0") else None
    w_cd = inp("w_in_cd", [D, 6144]) if need("A1") else None
    cos1_d = inp("cos1", [128, SEQ]) if need("A1") else None
    sin1_d = inp("sin1", [128, SEQ]) if need("A1") else None
    sel_d = inp("sel", [128, 2]) if need("A1") else None
    pb_d = inp("pb", [128, 1]) if need("B1") else None
    past_d = inp("past1", [128, 256]) if need("B1") else None
    wout = [inp("w_out_ab", [D, D]) if need("C0") else None, inp("w_out_cd", [D, D]) if need("C1") else None]
    wg = [inp("ffn_gate%d" % l, [D, FF]) if need("C%d" % l) else None for l in range(2)]
    wu = [inp("ffn_up%d" % l, [D, FF]) if need("C%d" % l) else None for l in range(2)]
    wd = [inp("ffn_down%d" % l, [FF, D]) if need("C%d" % l) else None for l in range(2)]
    xT0 = act("xT0", [16, 128, T], F32, "A0", ["C0"])
    q0 = act("q0", [24, 128, T], BF16, "A0", ["B0"])
    k0 = act("k0", [24, 128, SEQ], BF16, "A0", ["B0"])
    v0 = act("v0", [SEQ, 3072], BF16, "A0", ["B0"])
    cat0a = act("cat0a", [8, 128, T], BF16, "A0", ["C0"])
    cat0b = act("cat0b", [8, 128, T], BF16, "B0", ["C0"])
    xT1 = act("xT1", [16, 128, T], F32, "C0", ["A1"])
    xM = act("xM", [16, 128, 2048], F32, "A1", ["C1"])
    q1 = act("q1", [16, 128, 2048], BF16, "A1", ["B1"])
    k1 = act("k1", [16, 128, SEQ], BF16, "A1", ["B1"])
    v1 = act("v1", [SEQ, 2048], BF16, "A1", ["B1"])
    cat1 = act("cat1", [16, 128, 2048], BF16, "B1", ["C1"])
    yT = nc.dram_tensor("yT", [16, 128, T], F32, kind="Internal").ap() if need("C0", "C1") else None
    xT2 = nc.dram_tensor("xT2", [16, 128, 2048], F32, kind="Internal").ap() if need("C1") else None
    out_d = nc.dram_tensor("out", [2048, D], F32, kind="ExternalOutput").ap() if need("C1") else None
    K = Ctx(nc)
    with ExitStack() as st:
        setup_common(K, cb_d, cf_d)
        for ph in phases:
            if ph == "A0":
                phase_inproj(K, 0, x_d, xT0, gains_d, w_ab, cos_d, sin_d, q0, k0, v0, cat0a, poolw_d, psc_d, cf_d)
            elif ph == "B0":
                phase_dilated(K, q0, k0, v0, cat0b)
            elif ph == "C0":
                phase_post(K, 0, lambda k: (cat0a[k] if k < 8 else cat0b[k - 8]), xT0, xT1, yT, gains_d,
                           wout[0], wg[0], wu[0], wd[0], None, False)
            elif ph == "A1":
                phase_inproj(K, 1, None, xT1, gains_d, w_cd, cos1_d, sin1_d, q1, k1, v1, None, None, None, cf_d,
                             sel_d=sel_d, xM_d=xM)
            elif ph == "B1":
                phase_sb(K, q1, k1, v1, cat1, cf_d, pb_d)
                phase_moba(K, q1, k1, v1, cat1, cf_d, past_d)
            elif ph == "C1":
                phase_post(K, 1, lambda k: cat1[k], xM, xT2, yT, gains_d, wout[1], wg[1], wu[1], wd[1], out_d, True,
                           ntok=2048)
        K.S.emit(st)
    return nc, in_names


LAUNCHES = [list(ALL_PHASES)]
_cache = {}


def host_inputs(inputs):
    cb, cf, cos2, sinx = host_consts()
    base = {
        "cb": cb, "cf": cf, "cos2": cos2, "sinx": sinx,
        "norm_gains": np.ascontiguousarray(inputs["norm_gains"], dtype=np.float32),
        "w_in_ab": np.ascontiguousarray(inputs["w_in_ab"][0]),
        "pool_w": np.ascontiguousarray(inputs["pool_w"][0]),
        "pool_scale": np.ascontiguousarray(inputs["pool_scale"][0]),
        "w_out_ab": np.ascontiguousarray(inputs["w_out_ab"][0]),
        "w_in_cd": np.ascontiguousarray(inputs["w_in_cd"][0]),
        "w_out_cd": np.ascontiguousarray(inputs["w_out_cd"][0]),
    }
    for l in range(2):
        base["ffn_gate%d" % l] = np.ascontiguousarray(inputs["ffn_gate"][l])
        base["ffn_up%d" % l] = np.ascontiguousarray(inputs["ffn_up"][l])
        base["ffn_down%d" % l] = np.ascontiguousarray(inputs["ffn_down"][l])
    return base


def run_launches(launches, base, state, ncores):
    for phases in launches:
        key = tuple(phases)
        if key not in _cache:
            _cache[key] = build_program(phases)
        nc, names = _cache[key]
        in_maps = []
        for c in range(ncores):
            in_maps.append({n: (state[c][n] if n in state[c] else base[n]) for n in names})
        res = run_bass_kernel_spmd(nc, in_maps, core_ids=list(range(ncores)))
        for c in range(ncores):
            for n, v in res.results[c].items():
                state[c][n] = v
    return state


def core_tables(h, cos2, sinx):
    mine = slice(h * 2048, (h + 1) * 2048)
    cos1 = np.ascontiguousarray(np.concatenate([cos2[:, 0:2048], cos2[:, mine]], 1))
    sin1 = np.ascontiguousarray(np.concatenate([sinx[:, 0:2048], sinx[:, mine]], 1))
    sel = np.zeros((128, 2), np.float32)
    sel[:, h] = 1.0
    pb = np.full((128, 1), 0.0 if h == 1 else NEG, np.float32)
    past = np.full((16, 16), -1e30, np.float32)
    for i in range(16):
        G = i // 2
        if h == 1:
            past[i, 0:8] = 0.0
        past[i, 8:8 + G] = 0.0
    past1 = np.ascontiguousarray(np.broadcast_to(past.reshape(1, 256), (128, 256)))
    return {"cos1": cos1, "sin1": sin1, "sel": sel, "pb": pb, "past1": past1}


def kernel(**inputs):
    ncores = 8
    inputs = {k: np.asarray(v) for k, v in inputs.items()}
    base = host_inputs(inputs)
    x = np.ascontiguousarray(inputs["x"], dtype=np.float32)
    tabs = [core_tables(h, base["cos2"], base["sinx"]) for h in range(2)]
    state = []
    for c in range(ncores):
        st = {"x": x[c // 2]}
        st.update(tabs[c % 2])
        state.append(st)
    run_launches(LAUNCHES, base, state, ncores)
    out = np.stack([np.concatenate([state[2 * b]["out"], state[2 * b + 1]["out"]], 0) for b in range(4)], 0)
    return out.astype(np.float32)
```
